# Optimizing a Trainium2 kernel written in Bass

```python
import jax, jax.numpy as jnp
from jax import lax
import numpy as np

D_MODEL = 1024
BATCH = 1
SEQ = 16384
DEPTH = 4

HEAD_DIM = 64
N_MIXERS = 3
BAND_BLOCK = 128
A_HEADS = 16
A_KV_HEADS = 2
A_WINDOW = 128
B_HEADS = 16
B_PATTERNS = ((128, 1), (512, 4), (2048, 16))
B_GROUPS = len(B_PATTERNS)
C_HEADS = 16
C_BLOCK = 256
C_TOPK = 3
C_QCHUNK = 64
PEER_HEADS = 8
PEER_NKEYS = 128
PEER_EXPERTS = PEER_NKEYS * PEER_NKEYS
PEER_DKEY = 256
PEER_TOPK = 16
PEER_CHUNK = 128
PLE_DIM = 256
LN_EPS = 1e-5
DEEPNORM_ALPHA = (2 * DEPTH) ** 0.25
DEEPNORM_BETA = (8 * DEPTH) ** -0.25
N_LAYERS_A = (DEPTH + 2) // 3
N_LAYERS_B = (DEPTH + 1) // 3
N_LAYERS_C = DEPTH // 3
A_QKV = (A_HEADS + 2 * A_KV_HEADS) * HEAD_DIM
B_QKV = (B_GROUPS * B_HEADS + 2 * B_HEADS) * HEAD_DIM
C_QKV = 3 * C_HEADS * HEAD_DIM

kernel_name = "hybrid_swa_dilated_moba_peer_deepnorm"


def alibi_slopes(n):
    return jnp.exp2(-8.0 * jnp.arange(1, n + 1, dtype=jnp.float32) / n)


def layer_norm(x, g, b):
    xf = x.astype(jnp.float32)
    mu = xf.mean(-1, keepdims=True)
    var = jnp.mean(jnp.square(xf - mu), -1, keepdims=True)
    return ((xf - mu) * lax.rsqrt(var + LN_EPS) * g.astype(jnp.float32) + b.astype(jnp.float32)).astype(x.dtype)


def banded_attention(q, k, v, slopes, max_dist, dist_scale, sinks=None):
    n, s, hk, g, dh = q.shape
    nb = s // BAND_BLOCK
    qb = q.reshape(n, nb, BAND_BLOCK, hk, g, dh)

    def with_prev(t):
        tb = t.reshape(n, nb, BAND_BLOCK, hk, dh)
        prev = jnp.pad(tb, ((0, 0), (1, 0), (0, 0), (0, 0), (0, 0)))[:, :-1]
        return jnp.concatenate([prev, tb], axis=2)

    kb, vb = with_prev(k), with_prev(v)
    scores = jnp.einsum('nbqhgd,nbkhd->nbhgqk', qb, kb).astype(jnp.float32) * (dh ** -0.5)
    dist = (jnp.arange(BAND_BLOCK)[:, None] + BAND_BLOCK) - jnp.arange(2 * BAND_BLOCK)[None, :]
    valid = (dist >= 0) & (dist <= max_dist)
    first = (jnp.arange(nb) == 0)[:, None, None] & (jnp.arange(2 * BAND_BLOCK) < BAND_BLOCK)[None, None, :]
    valid = valid[None] & ~first
    bias = -(slopes.reshape(hk, g)[:, :, None, None] * (dist_scale * dist.astype(jnp.float32)))
    scores = jnp.where(valid[None, :, None, None], scores + bias[None, None], -jnp.inf)
    m = scores.max(-1)
    if sinks is not None:
        sink = sinks.reshape(hk, g).astype(jnp.float32)[None, None, :, :, None]
        m = jnp.maximum(m, sink)
    e = jnp.exp(scores - m[..., None])
    denom = e.sum(-1)
    if sinks is not None:
        denom = denom + jnp.exp(sink - m)
    probs = (e / denom[..., None]).astype(v.dtype)
    o = jnp.einsum('nbhgqk,nbkhd->nbqhgd', probs, vb).reshape(n, s, hk, g, dh)
    lse = (m + jnp.log(denom)).transpose(0, 1, 4, 2, 3).reshape(n, s, hk, g)
    return o, lse


def mixer_a(x, w_qkv, sinks, w_o):
    b, s, _ = x.shape
    grp = A_HEADS // A_KV_HEADS
    qkv = x @ w_qkv
    q, k, v = jnp.split(qkv, [A_HEADS * HEAD_DIM, (A_HEADS + A_KV_HEADS) * HEAD_DIM], axis=-1)
    q = q.reshape(b, s, A_KV_HEADS, grp, HEAD_DIM)
    k = k.reshape(b, s, A_KV_HEADS, HEAD_DIM)
    v = v.reshape(b, s, A_KV_HEADS, HEAD_DIM)
    o, _ = banded_attention(q, k, v, alibi_slopes(A_HEADS), A_WINDOW - 1, 1.0, sinks)
    return o.reshape(b, s, A_HEADS * HEAD_DIM) @ w_o


def dilate(t, r):
    b, sp = t.shape[:2]
    t = jnp.moveaxis(t.reshape(b, sp // r, r, *t.shape[2:]), 2, 1)
    return t.reshape(b * r, sp // r, *t.shape[3:])


def undilate(t, b, r):
    sr = t.shape[1]
    t = jnp.moveaxis(t.reshape(b, r, sr, *t.shape[2:]), 1, 2)
    return t.reshape(b, sr * r, *t.shape[3:])


def mixer_b(x, w_qkv, w_o):
    b, s, _ = x.shape
    hd = B_HEADS * HEAD_DIM
    qkv = x @ w_qkv
    qs = qkv[..., :B_GROUPS * hd].reshape(b, s, B_GROUPS, B_HEADS, HEAD_DIM)
    k = qkv[..., B_GROUPS * hd:(B_GROUPS + 1) * hd].reshape(b, s, B_HEADS, HEAD_DIM)
    v = qkv[..., (B_GROUPS + 1) * hd:].reshape(b, s, B_HEADS, HEAD_DIM)
    slopes = alibi_slopes(B_HEADS)
    outs, lses = [], []
    for gi, (w, r) in enumerate(B_PATTERNS):
        span = r * BAND_BLOCK
        sp = -(-s // span) * span
        pad = ((0, 0), (0, sp - s), (0, 0), (0, 0))
        qg = dilate(jnp.pad(qs[:, :, gi], pad), r)[:, :, :, None]
        kg = dilate(jnp.pad(k, pad), r)
        vg = dilate(jnp.pad(v, pad), r)
        o, lse = banded_attention(qg, kg, vg, slopes, w // r, float(r))
        outs.append(undilate(o[:, :, :, 0], b, r)[:, :s])
        lses.append(undilate(lse[..., 0], b, r)[:, :s])
    wts = jax.nn.softmax(jnp.stack(lses, 0), axis=0)
    o = jnp.einsum('gbsh,gbshd->bshd', wts.astype(x.dtype), jnp.stack(outs, 0))
    return o.reshape(b, s, hd) @ w_o


def mixer_c(x, w_qkv, w_o):
    b, s, _ = x.shape
    hd = C_HEADS * HEAD_DIM
    scale = HEAD_DIM ** -0.5
    sp = -(-s // C_BLOCK) * C_BLOCK
    nblk = sp // C_BLOCK
    qkv = jnp.pad(x @ w_qkv, ((0, 0), (0, sp - s), (0, 0)))
    q, k, v = jnp.split(qkv, 3, axis=-1)
    q = q.reshape(b, sp, C_HEADS, HEAD_DIM)
    kb = k.reshape(b, nblk, C_BLOCK, C_HEADS, HEAD_DIM)
    vb = v.reshape(b, nblk, C_BLOCK, C_HEADS, HEAD_DIM)
    kmean = kb.mean(axis=2)
    gate = jnp.einsum('bshd,bnhd->bhsn', q, kmean).astype(jnp.float32)
    qblk = jnp.arange(sp) // C_BLOCK
    past = jnp.arange(nblk)[None, :] < qblk[:, None]
    gate = jnp.where(past[None, None], gate, -jnp.inf)
    topk = min(C_TOPK, nblk)
    _, sel = lax.top_k(gate, topk)
    sel_valid = jnp.arange(topk)[None, :] < qblk[:, None]
    kbh = kb.transpose(0, 3, 1, 2, 4)
    vbh = vb.transpose(0, 3, 1, 2, 4)
    slopes = alibi_slopes(C_HEADS)
    bi = jnp.arange(b)[:, None, None, None]
    hi = jnp.arange(C_HEADS)[None, :, None, None]

    def attend_chunk(c):
        t0 = c * C_QCHUNK
        qc = lax.dynamic_slice_in_dim(q, t0, C_QCHUNK, axis=1)
        selc = lax.dynamic_slice_in_dim(sel, t0, C_QCHUNK, axis=2)
        validc = lax.dynamic_slice_in_dim(sel_valid, t0, C_QCHUNK, axis=0)
        ob = t0 // C_BLOCK
        k_own = lax.dynamic_index_in_dim(kb, ob, axis=1, keepdims=False)
        v_own = lax.dynamic_index_in_dim(vb, ob, axis=1, keepdims=False)
        kg = kbh[bi, hi, selc]
        vg = vbh[bi, hi, selc]
        tq = t0 + jnp.arange(C_QCHUNK)
        s_sel = jnp.einsum('bqhd,bhqkjd->bhqkj', qc, kg).astype(jnp.float32) * scale
        pos_sel = selc[..., None] * C_BLOCK + jnp.arange(C_BLOCK)
        d_sel = (tq[:, None, None] - pos_sel).astype(jnp.float32)
        s_sel = jnp.where(validc[None, None, :, :, None], s_sel - slopes[:, None, None, None] * d_sel, -jnp.inf)
        s_own = jnp.einsum('bqhd,bjhd->bhqj', qc, k_own).astype(jnp.float32) * scale
        d_own = tq[:, None] - (ob * C_BLOCK + jnp.arange(C_BLOCK))[None, :]
        s_own = jnp.where((d_own >= 0)[None, None], s_own - slopes[:, None, None] * d_own.astype(jnp.float32), -jnp.inf)
        scores = jnp.concatenate([s_sel.reshape(b, C_HEADS, C_QCHUNK, topk * C_BLOCK), s_own], axis=-1)
        probs = jax.nn.softmax(scores, axis=-1).astype(v.dtype)
        p_sel = probs[..., :topk * C_BLOCK].reshape(b, C_HEADS, C_QCHUNK, topk, C_BLOCK)
        p_own = probs[..., topk * C_BLOCK:]
        return (jnp.einsum('bhqkj,bhqkjd->bqhd', p_sel, vg)
                + jnp.einsum('bhqj,bjhd->bqhd', p_own, v_own))

    o = lax.map(attend_chunk, jnp.arange(sp // C_QCHUNK))
    o = jnp.moveaxis(o, 0, 1).reshape(b, sp, hd)[:, :s]
    return o @ w_o


def peer(x, w_q, subkeys, u, v):
    b, s, d = x.shape
    n_tok = b * s
    t = x.reshape(n_tok, d)
    q = (t @ w_q).astype(jnp.float32).reshape(n_tok, PEER_HEADS, 2, PEER_DKEY // 2)
    sc = jnp.einsum('thcd,ckd->thck', q, subkeys.astype(jnp.float32))
    v1, i1 = lax.top_k(sc[:, :, 0], PEER_TOPK)
    v2, i2 = lax.top_k(sc[:, :, 1], PEER_TOPK)
    cand = (v1[..., :, None] + v2[..., None, :]).reshape(n_tok, PEER_HEADS, PEER_TOPK * PEER_TOPK)
    cidx = (i1[..., :, None] * PEER_NKEYS + i2[..., None, :]).reshape(n_tok, PEER_HEADS, PEER_TOPK * PEER_TOPK)
    best, pos = lax.top_k(cand, PEER_TOPK)
    experts = jnp.take_along_axis(cidx, pos, axis=-1).reshape(n_tok, PEER_HEADS * PEER_TOPK)
    gates = jax.nn.softmax(best, axis=-1).reshape(n_tok, PEER_HEADS * PEER_TOPK)

    def chunk(c):
        t0 = c * PEER_CHUNK
        xc = lax.dynamic_slice_in_dim(t, t0, PEER_CHUNK, axis=0)
        ec = lax.dynamic_slice_in_dim(experts, t0, PEER_CHUNK, axis=0)
        gc = lax.dynamic_slice_in_dim(gates, t0, PEER_CHUNK, axis=0)
        h = jnp.einsum('cd,ced->ce', xc, u[ec])
        a = (jax.nn.gelu(h.astype(jnp.float32), approximate=False) * gc).astype(x.dtype)
        return jnp.einsum('ce,ced->cd', a, v[ec])

    return lax.map(chunk, jnp.arange(n_tok // PEER_CHUNK)).reshape(b, s, d)


def per_layer_embedding(x, p_i, w_p, w_g, b_g):
    gate = jax.nn.sigmoid((x @ w_g + b_g).astype(jnp.float32)).astype(x.dtype)
    return x + gate * (p_i @ w_p)


def setup_inputs(seed: int = 0) -> dict:
    key = jax.random.key(seed)
    ks = jax.random.split(key, 20)
    f32 = jnp.float32
    d = D_MODEL

    def nrm(k, shape, scale):
        return jax.random.normal(k, shape, f32) * scale

    def value_scaled(w, v_start):
        col = jnp.where(jnp.arange(w.shape[-1]) >= v_start, DEEPNORM_BETA, 1.0).astype(f32)
        return w * col

    hd = B_HEADS * HEAD_DIM
    return {
        "x": nrm(ks[0], (BATCH, SEQ, d), 1.0),
        "p": nrm(ks[1], (DEPTH, BATCH, SEQ, PLE_DIM), 1.0),
        "a_w_qkv": value_scaled(nrm(ks[2], (N_LAYERS_A, d, A_QKV), d ** -0.5), (A_HEADS + A_KV_HEADS) * HEAD_DIM),
        "a_sinks": nrm(ks[3], (N_LAYERS_A, A_HEADS), 0.5),
        "a_w_o": nrm(ks[4], (N_LAYERS_A, A_HEADS * HEAD_DIM, d), DEEPNORM_BETA * (A_HEADS * HEAD_DIM) ** -0.5),
        "b_w_qkv": value_scaled(nrm(ks[5], (N_LAYERS_B, d, B_QKV), d ** -0.5), (B_GROUPS + 1) * hd),
        "b_w_o": nrm(ks[6], (N_LAYERS_B, hd, d), DEEPNORM_BETA * hd ** -0.5),
        "c_w_qkv": value_scaled(nrm(ks[7], (N_LAYERS_C, d, C_QKV), d ** -0.5), 2 * C_HEADS * HEAD_DIM),
        "c_w_o": nrm(ks[8], (N_LAYERS_C, C_HEADS * HEAD_DIM, d), DEEPNORM_BETA * (C_HEADS * HEAD_DIM) ** -0.5),
        "ln1_g": 1.0 + nrm(ks[9], (DEPTH, d), 0.02),
        "ln1_b": nrm(ks[10], (DEPTH, d), 0.02),
        "ln2_g": 1.0 + nrm(ks[11], (DEPTH, d), 0.02),
        "ln2_b": nrm(ks[12], (DEPTH, d), 0.02),
        "peer_w_q": nrm(ks[13], (DEPTH, d, PEER_HEADS * PEER_DKEY), d ** -0.5),
        "peer_subkeys": nrm(ks[14], (DEPTH, 2, PEER_NKEYS, PEER_DKEY // 2), (PEER_DKEY // 2) ** -0.5),
        "peer_u": nrm(ks[15], (DEPTH, PEER_EXPERTS, d), d ** -0.5),
        "peer_v": nrm(ks[16], (DEPTH, PEER_EXPERTS, d), DEEPNORM_BETA * PEER_HEADS ** -0.5),
        "ple_w": nrm(ks[17], (DEPTH, PLE_DIM, d), PLE_DIM ** -0.5),
        "ple_gate_w": nrm(ks[18], (DEPTH, d, d), d ** -0.5),
        "ple_gate_b": nrm(ks[19], (DEPTH, d), 0.02),
    }


def reference(x, p, a_w_qkv, a_sinks, a_w_o, b_w_qkv, b_w_o, c_w_qkv, c_w_o,
              ln1_g, ln1_b, ln2_g, ln2_b, peer_w_q, peer_subkeys, peer_u, peer_v,
              ple_w, ple_gate_w, ple_gate_b):
    for i in range(DEPTH):
        kind, j = i % N_MIXERS, i // N_MIXERS
        if kind == 0:
            y = mixer_a(x, a_w_qkv[j], a_sinks[j], a_w_o[j])
        elif kind == 1:
            y = mixer_b(x, b_w_qkv[j], b_w_o[j])
        else:
            y = mixer_c(x, c_w_qkv[j], c_w_o[j])
        x = layer_norm(DEEPNORM_ALPHA * x + y, ln1_g[i], ln1_b[i])
        y = peer(x, peer_w_q[i], peer_subkeys[i], peer_u[i], peer_v[i])
        x = layer_norm(DEEPNORM_ALPHA * x + y, ln2_g[i], ln2_b[i])
        x = per_layer_embedding(x, p[i], ple_w[i], ple_gate_w[i], ple_gate_b[i])
    return x
```

```python
import numpy as np
from contextlib import ExitStack
import concourse.bass as bass
import concourse.mybir as mybir
from concourse.bass_utils import run_bass_kernel_spmd

F32 = mybir.dt.float32
I32 = mybir.dt.int32
U32 = mybir.dt.uint32
ALU = mybir.AluOpType
AF = mybir.ActivationFunctionType
AX = mybir.AxisListType

NCORES = 8
SEQ = 16384
D = 1024
T = SEQ // NCORES
DEPTH = 4
HD = 64
ALPHA = (2 * DEPTH) ** 0.25
LN_EPS = 1e-5
ENG = ("pe", "act", "dve", "pool", "sp")


class Prog:
    def __init__(self, nc):
        self.nc = nc
        self.ops = {e: [] for e in ENG}
        self.nseq = {e: 0 for e in ENG}
        self.last_w = {}
        self.reads = {}
        self.dma_keys = {}
        self.waited = {e: {} for e in ENG}
        self.st = ExitStack()
        self.npsum = 0

    def sb(self, name, shape, dt=F32):
        return self.st.enter_context(self.nc.sbuf_tensor(name, list(shape), dt))

    def psum(self, name, shape=(128, 512), dt=F32):
        return self.st.enter_context(self.nc.psum_tensor(name, list(shape), dt))

    def _deps(self, eng, reads, writes, is_dma):
        deps = []
        for r in reads:
            if r in self.last_w:
                deps.append(self.last_w[r])
        for w in writes:
            if w in self.last_w:
                deps.append(self.last_w[w])
            deps.extend(self.reads.get(w, ()))
        if eng == "pe" and not is_dma:
            deps = [d for d in deps if not (d["eng"] == "pe" and not d["dma"])]
        return deps

    def _commit(self, op, reads, writes):
        for r in reads:
            self.reads.setdefault(r, []).append(op)
        for w in writes:
            self.last_w[w] = op
            self.reads[w] = []

    def _waits_for(self, eng, deps):
        waits = {}
        for d in deps:
            sem, val = d["sem"], d["val"]
            if self.waited[eng].get(sem, 0) >= val:
                continue
            if waits.get(sem, 0) < val:
                waits[sem] = val
        for s, v in waits.items():
            self.waited[eng][s] = v
        return list(waits.items())

    def op(self, eng, fn, reads=(), writes=()):
        deps = self._deps(eng, reads, writes, False)
        self.nseq[eng] += 1
        o = {"eng": eng, "fn": fn, "sem": "c_" + eng, "val": self.nseq[eng], "inc": 1, "dma": False}
        o["waits"] = self._waits_for(eng, deps)
        self.ops[eng].append(o)
        self._commit(o, reads, writes)
        return o

    def dma(self, eng, fn, key, reads=(), writes=()):
        deps = self._deps(eng, reads, writes, True)
        k = self.dma_keys.setdefault(key, ["d_%d" % len(self.dma_keys), 0, None])
        if k[2] is not None:
            deps.append(k[2])
        k[1] += 16
        o = {"eng": eng, "fn": fn, "sem": k[0], "val": k[1], "inc": 16, "dma": True}
        o["waits"] = self._waits_for(eng, deps)
        k[2] = o
        self.ops[eng].append(o)
        self._commit(o, reads, writes)
        return o

    def mm(self, out, lhsT, rhs, start, stop, r, w):
        return self.op("pe", lambda e: e.matmul(out, lhsT=lhsT, rhs=rhs, start=start, stop=stop), r, w)

    def tr(self, out, in_, ident, r, w):
        return self.op("pe", lambda e: e.transpose(out, in_, ident), r, w)

    def act(self, out, in_, func, r, w, scale=1.0, bias=None, accum=None):
        def f(e):
            kw = {}
            if bias is not None:
                kw["bias"] = bias
            if accum is not None:
                kw["accum_out"] = accum
            return e.activation(out=out, in_=in_, func=func, scale=scale, **kw)
        return self.op("act", f, r, w)

    def tt(self, eng, out, in0, in1, op, r, w):
        return self.op(eng, lambda e: e.tensor_tensor(out=out, in0=in0, in1=in1, op=op), r, w)

    def ts(self, eng, out, in0, s1, s2, op0, op1, r, w, accum=None):
        def f(e):
            if accum is not None:
                return e.tensor_scalar(out=out, in0=in0, scalar1=s1, scalar2=s2, op0=op0, op1=op1, accum_out=accum)
            if s2 is None:
                return e.tensor_scalar(out=out, in0=in0, scalar1=s1, scalar2=None, op0=op0)
            return e.tensor_scalar(out=out, in0=in0, scalar1=s1, scalar2=s2, op0=op0, op1=op1)
        return self.op(eng, f, r, w)

    def stt(self, out, in0, scalar, in1, op0, op1, r, w, accum=None):
        def f(e):
            if accum is not None:
                return e.scalar_tensor_tensor(out=out, in0=in0, scalar=scalar, in1=in1, op0=op0, op1=op1, accum_out=accum)
            return e.scalar_tensor_tensor(out=out, in0=in0, scalar=scalar, in1=in1, op0=op0, op1=op1)
        return self.op("dve", f, r, w)

    def copy(self, eng, out, in_, r, w):
        if eng == "act":
            return self.op("act", lambda e: e.copy(out=out, in_=in_), r, w)
        return self.op(eng, lambda e: e.tensor_copy(out=out, in_=in_), r, w)

    def memset(self, eng, ap, val, w):
        return self.op(eng, lambda e: e.memset(ap, val), (), w)

    def ld(self, out, in_, key, r, w, eng="sp"):
        return self.dma(eng, lambda e: e.dma_start(out=out, in_=in_), key, r, w)

    def emit(self):
        nc = self.nc
        final_ops = [k[2] for k in self.dma_keys.values()]
        names = ["c_" + e for e in ENG] + [k[0] for k in self.dma_keys.values()]
        sems = {n: self.st.enter_context(nc.semaphore(n)) for n in names}
        fin = [(o["sem"], o["val"]) for o in final_ops]
        with nc.Block() as block:
            def run(engine, e):
                for o in self.ops[e]:
                    for s, v in o["waits"]:
                        engine.wait_ge(sems[s], v)
                    o["fn"](engine).then_inc(sems[o["sem"]], o["inc"])
                if e == "sp":
                    for s, v in fin:
                        engine.wait_ge(sems[s], v)

            @block.tensor
            def _(eng):
                run(eng, "pe")

            @block.scalar
            def _(eng):
                run(eng, "act")

            @block.vector
            def _(eng):
                run(eng, "dve")

            @block.gpsimd
            def _(eng):
                run(eng, "pool")

            @block.sync
            def _(eng):
                run(eng, "sp")
        self.st.close()


def new_nc():
    return bass.Bass("TRN2", target_bir_lowering=False)


def din(nc, name, shape, dt=F32):
    return nc.dram_tensor(name, list(shape), dt, kind="ExternalInput").ap()


def dout(nc, name, shape, dt=F32):
    return nc.dram_tensor(name, list(shape), dt, kind="ExternalOutput").ap()


_CACHE = {}


def run(key, builder, in_maps):
    if key not in _CACHE:
        _CACHE[key] = builder()
    res = run_bass_kernel_spmd(_CACHE[key], in_maps, core_ids=list(range(NCORES)))
    return res.results


def build_proj(din_, nf, nv):
    nc = new_nc()
    kc_n = din_ // 128
    xT = din(nc, "xT", [din_, T])
    W = din(nc, "W", [din_, nf + nv])
    yT = dout(nc, "yT", [max(nf, 128), T]) if nf else None
    v = dout(nc, "v", [T, nv]) if nv else None
    P = Prog(nc)
    xt = P.sb("xt", [128, kc_n, T])
    wt = [P.sb("wt%d" % i, [128, kc_n, 512]) for i in range(2)]
    stg = [P.sb("stg%d" % i, [128, T]) for i in range(2)]
    ps = [P.psum("ps%d" % i) for i in range(4)]
    for kc in range(kc_n):
        P.ld(xt[:, kc, :], xT[kc * 128:(kc + 1) * 128, :], "xt%d" % kc, (), ["xt%d" % kc])
    xr = ["xt%d" % kc for kc in range(kc_n)]
    Wv = W.rearrange("(kc p) n -> p kc n", p=128)
    nblk = (nf + nv + 511) // 512
    pi = 0
    si = 0
    ei = 0
    for j in range(nblk):
        c0 = j * 512
        cw = min(512, nf + nv - c0)
        b = j % 2
        P.ld(wt[b][:, :, :cw], Wv[:, :, c0:c0 + cw], "wt%d" % b, (), ["wt%d" % b])
        for n in range(c0, min(c0 + cw, nf), 128):
            s = si % 2
            si += 1
            for tg in range(T // 512):
                p = pi % 4
                pi += 1
                for kc in range(kc_n):
                    P.mm(ps[p][:, :], wt[b][:, kc, n - c0:n - c0 + 128], xt[:, kc, tg * 512:(tg + 1) * 512],
                         kc == 0, kc == kc_n - 1, ["wt%d" % b, xr[kc]], ["ps%d" % p])
                e = "act" if ei % 2 == 0 else "dve"
                ei += 1
                P.copy(e, stg[s][:, tg * 512:(tg + 1) * 512], ps[p][:, :], ["ps%d" % p], ["stg%d_%d" % (s, tg)])
            P.ld(yT[n:n + 128, :], stg[s][:, :], "stg%d" % s,
                 ["stg%d_%d" % (s, tg) for tg in range(T // 512)], ())
        v0 = max(c0, nf)
        if v0 < c0 + cw:
            vw = c0 + cw - v0
            for tt in range(T // 128):
                p = pi % 4
                pi += 1
                s = si % 2
                si += 1
                for kc in range(kc_n):
                    P.mm(ps[p][:, :vw], xt[:, kc, tt * 128:(tt + 1) * 128], wt[b][:, kc, v0 - c0:v0 - c0 + vw],
                         kc == 0, kc == kc_n - 1, ["wt%d" % b, xr[kc]], ["ps%d" % p])
                e = "act" if ei % 2 == 0 else "dve"
                ei += 1
                P.copy(e, stg[s][:, :vw], ps[p][:, :vw], ["ps%d" % p], ["stg%d_0" % s])
                P.ld(v[tt * 128:(tt + 1) * 128, v0 - nf:v0 - nf + vw], stg[s][:, :vw], "stg%d" % s,
                     ["stg%d_0" % s], ())
    P.emit()
    return nc


def ln_tok(P, z, zr, out, outw, g_bc, b_bc, junk, st, tag):
    P.act(junk[:, :], z[:, :], AF.Copy, zr, ["junk" + tag, "st0" + tag], accum=st[:, 0:1])
    P.act(junk[:, :], z[:, :], AF.Square, zr, ["junk" + tag, "st1" + tag], accum=st[:, 1:2])
    sr = ["st0" + tag, "st1" + tag]
    P.ts("dve", st[:, 2:4], st[:, 0:2], 1.0 / D, None, ALU.mult, None, sr, ["st2" + tag])
    P.tt("dve", st[:, 4:5], st[:, 2:3], st[:, 2:3], ALU.mult, ["st2" + tag], ["st4" + tag])
    P.tt("dve", st[:, 5:6], st[:, 3:4], st[:, 4:5], ALU.subtract, ["st2" + tag, "st4" + tag], ["st5" + tag])
    P.ts("dve", st[:, 5:6], st[:, 5:6], LN_EPS, None, ALU.add, None, ["st5" + tag], ["st5" + tag])
    P.act(st[:, 6:7], st[:, 5:6], AF.Sqrt, ["st5" + tag], ["st6" + tag])
    P.op("dve", lambda e: e.reciprocal(out=st[:, 7:8], in_=st[:, 6:7]), ["st6" + tag], ["st7" + tag])
    P.ts("dve", out[:, :], z[:, :], st[:, 2:3], st[:, 7:8], ALU.subtract, ALU.mult,
         zr + ["st2" + tag, "st7" + tag], outw)
    P.tt("dve", out[:, :], out[:, :], g_bc[:, :], ALU.mult, outw + ["gb"], outw)
    P.tt("dve", out[:, :], out[:, :], b_bc[:, :], ALU.add, outw + ["gb"], outw)


def build_post():
    nc = new_nc()
    OT = din(nc, "OT", [D, T])
    X = din(nc, "X", [T, D])
    Wo = din(nc, "Wo", [D, D])
    G = din(nc, "G", [128, D])
    B = din(nc, "B", [128, D])
    Y = dout(nc, "Y", [T, D])
    P = Prog(nc)
    ot = P.sb("ot", [128, 8, T])
    wo = P.sb("wo", [128, 8, D])
    g_bc = P.sb("g_bc", [128, D])
    b_bc = P.sb("b_bc", [128, D])
    xt = [P.sb("x%d" % i, [128, D]) for i in range(2)]
    zt = [P.sb("z%d" % i, [128, D]) for i in range(2)]
    o_t = [P.sb("o%d" % i, [128, D]) for i in range(2)]
    junk = P.sb("junk", [128, D])
    st = [P.sb("st%d" % i, [128, 8]) for i in range(2)]
    ps = [P.psum("ps%d" % i) for i in range(4)]
    for kc in range(8):
        P.ld(ot[:, kc, :], OT[kc * 128:(kc + 1) * 128, :], "ot%d" % kc, (), ["ot%d" % kc])
    P.ld(wo[:, :, :], Wo.rearrange("(kc p) n -> p kc n", p=128), "wo", (), ["wo"])
    P.ld(g_bc[:, :], G[:, :], "g", (), ["gb"])
    P.ld(b_bc[:, :], B[:, :], "b", (), ["gb"])
    for tt in range(T // 128):
        b = tt % 2
        P.ld(xt[b][:, :], X[tt * 128:(tt + 1) * 128, :], "x%d" % b, (), ["x%d" % b])
        for hf in range(2):
            p = (tt * 2 + hf) % 4
            for kc in range(8):
                P.mm(ps[p][:, :], ot[:, kc, tt * 128:(tt + 1) * 128], wo[:, kc, hf * 512:(hf + 1) * 512],
                     kc == 0, kc == 7, ["ot%d" % kc, "wo"], ["ps%d" % p])
            P.stt(zt[b][:, hf * 512:(hf + 1) * 512], xt[b][:, hf * 512:(hf + 1) * 512], ALPHA, ps[p][:, :],
                  ALU.mult, ALU.add, ["x%d" % b, "ps%d" % p], ["z%d_%d" % (b, hf)])
        ln_tok(P, zt[b], ["z%d_0" % b, "z%d_1" % b], o_t[b], ["o%d" % b], g_bc, b_bc, junk, st[b], str(b))
        P.ld(Y[tt * 128:(tt + 1) * 128, :], o_t[b][:, :], "o%d" % b, ["o%d" % b], ())
    P.emit()
    return nc


def build_ple():
    nc = new_nc()
    XT = din(nc, "XT", [D, T])
    PT = din(nc, "PT", [256, T])
    Wg = din(nc, "Wg", [D, D])
    Wp = din(nc, "Wp", [256, D])
    BG = din(nc, "BG", [128, 8])
    YT = dout(nc, "YT", [D, T])
    P = Prog(nc)
    xt = P.sb("xt", [128, 8, T])
    pt = P.sb("pt", [128, 2, T])
    wg = P.sb("wg", [128, 8, D])
    wp = P.sb("wp", [128, 2, D])
    bg = P.sb("bg", [128, 8])
    gs = [P.sb("gs%d" % i, [128, 512]) for i in range(2)]
    stg = [P.sb("stg%d" % i, [128, T]) for i in range(2)]
    ps = [P.psum("ps%d" % i) for i in range(4)]
    for kc in range(8):
        P.ld(xt[:, kc, :], XT[kc * 128:(kc + 1) * 128, :], "xt%d" % kc, (), ["xt%d" % kc])
    for kc in range(2):
        P.ld(pt[:, kc, :], PT[kc * 128:(kc + 1) * 128, :], "pt%d" % kc, (), ["pt%d" % kc])
    P.ld(wg[:, :, :], Wg.rearrange("(kc p) n -> p kc n", p=128), "wg", (), ["wg"])
    P.ld(wp[:, :, :], Wp.rearrange("(kc p) n -> p kc n", p=128), "wp", (), ["wp"])
    P.ld(bg[:, :], BG[:, :], "bg", (), ["bg"])
    it = 0
    for fo in range(8):
        s = fo % 2
        for tg in range(T // 512):
            pa = (it * 2) % 4
            pb = (it * 2 + 1) % 4
            gb = it % 2
            it += 1
            tsl = slice(tg * 512, (tg + 1) * 512)
            for kc in range(8):
                P.mm(ps[pa][:, :], wg[:, kc, fo * 128:(fo + 1) * 128], xt[:, kc, tsl], kc == 0, kc == 7,
                     ["wg", "xt%d" % kc], ["ps%d" % pa])
            for kc in range(2):
                P.mm(ps[pb][:, :], wp[:, kc, fo * 128:(fo + 1) * 128], pt[:, kc, tsl], kc == 0, kc == 1,
                     ["wp", "pt%d" % kc], ["ps%d" % pb])
            P.act(gs[gb][:, :], ps[pa][:, :], AF.Sigmoid, ["ps%d" % pa, "bg"], ["gs%d" % gb], bias=bg[:, fo:fo + 1])
            P.tt("dve", gs[gb][:, :], gs[gb][:, :], ps[pb][:, :], ALU.mult, ["gs%d" % gb, "ps%d" % pb], ["gs%d" % gb])
            P.tt("dve", stg[s][:, tsl], gs[gb][:, :], xt[:, fo, tsl], ALU.add, ["gs%d" % gb, "xt%d" % fo],
                 ["stg%d_%d" % (s, tg)])
        P.ld(YT[fo * 128:(fo + 1) * 128, :], stg[s][:, :], "stg%d" % s,
             ["stg%d_%d" % (s, tg) for tg in range(T // 512)], ())
    P.emit()
    return nc


def build_peer_sel(cut=99, nh=8):
    nc = new_nc()
    XT = din(nc, "XT", [D, T])
    Wq = din(nc, "Wq", [D, 2048])
    SKT = din(nc, "SKT", [128, 256])
    IOTA = din(nc, "IOTA", [128, 256])
    EI = dout(nc, "EI", [T, 128], I32)
    GT = dout(nc, "GT", [T, 128])
    NT = T // 128
    P = Prog(nc)
    xt = P.sb("xt", [128, 8, T])
    wq = [P.sb("wq%d" % i, [128, 8, 256]) for i in range(2)]
    skt = P.sb("skt", [128, 256])
    iota = P.sb("iota", [128, 256])
    qT = [P.sb("qT%d" % i, [128, 2, T]) for i in range(2)]
    eif = P.sb("eif", [128, NT, 128])
    eii = P.sb("eii", [128, NT, 128], I32)
    gts = P.sb("gts", [128, NT, 128])
    ps = [P.psum("ps%d" % i) for i in range(4)]
    NB = 2
    sc = [P.sb("sc%d" % i, [128, 256]) for i in range(NB)]
    tmp = [P.sb("tmp%d" % i, [128, 256]) for i in range(NB)]
    m8 = [P.sb("m8%d" % i, [128, 2, 16]) for i in range(NB)]
    i8 = [P.sb("i8%d" % i, [128, 2, 16], U32) for i in range(NB)]
    i_f = [P.sb("if%d" % i, [128, 2, 16]) for i in range(NB)]
    cand = [P.sb("cand%d" % i, [128, 16, 16]) for i in range(NB)]
    cidx = [P.sb("cidx%d" % i, [128, 16, 16]) for i in range(NB)]
    junk = [P.sb("junk%d" % i, [128, 256]) for i in range(NB)]
    b16 = [P.sb("b16%d" % i, [128, 16]) for i in range(NB)]
    sm = [P.sb("sm%d" % i, [128, 4]) for i in range(NB)]
    ex = [P.sb("ex%d" % i, [128, 16]) for i in range(NB)]
    for kc in range(8):
        P.ld(xt[:, kc, :], XT[kc * 128:(kc + 1) * 128, :], "xt%d" % kc, (), ["xt%d" % kc])
    P.ld(skt[:, :], SKT[:, :], "skt", (), ["skt"])
    P.ld(iota[:, :], IOTA[:, :], "iota", (), ["iota"])
    p16 = [P.sb("p16%d" % i, [128, 16], U32) for i in range(NB)]
    p16f = [P.sb("p16f%d" % i, [128, 16]) for i in range(NB)]
    Wv = Wq.rearrange("(kc p) n -> p kc n", p=128)
    pi = 0
    it = 0
    for h in range(nh):
        hb = h % 2
        P.ld(wq[hb][:, :, :], Wv[:, :, h * 256:(h + 1) * 256], "wq%d" % hb, (), ["wq%d" % hb])
        for c in range(2):
            for tg in range(T // 512):
                p = pi % 4
                pi += 1
                for kc in range(8):
                    P.mm(ps[p][:, :], wq[hb][:, kc, c * 128:(c + 1) * 128], xt[:, kc, tg * 512:(tg + 1) * 512],
                         kc == 0, kc == 7, ["wq%d" % hb, "xt%d" % kc], ["ps%d" % p])
                P.copy("act", qT[hb][:, c, tg * 512:(tg + 1) * 512], ps[p][:, :], ["ps%d" % p],
                       ["qT%d_%d_%d" % (hb, c, tg)])
        for tt in range(NT):
            k = it % NB
            it += 1
            K_ = str(k)
            p = pi % 4
            pi += 1
            tg = tt // 4
            for c in range(2):
                P.mm(ps[p][:, c * 128:(c + 1) * 128], qT[hb][:, c, tt * 128:(tt + 1) * 128], skt[:, c * 128:(c + 1) * 128],
                     True, True, ["qT%d_%d_%d" % (hb, c, tg), "skt"], ["ps%d" % p])
            P.copy("act", sc[k][:, :], ps[p][:, 0:256], ["ps%d" % p], ["sc" + K_])
            if cut < 1:
                continue
            for c in range(2):
                s_ = sc[k][:, c * 128:(c + 1) * 128]
                t_ = tmp[k][:, c * 128:(c + 1) * 128]
                P.op("dve", lambda e, o=m8[k][:, c, 0:8], i=s_: e.max(out=o, in_=i), ["sc" + K_], ["m8a" + K_])
                P.op("dve", lambda e, o=t_, r=m8[k][:, c, 0:8], i=s_: e.match_replace(out=o, in_to_replace=r, in_values=i, imm_value=-1e30),
                     ["sc" + K_, "m8a" + K_], ["tmp" + K_])
                P.op("dve", lambda e, o=m8[k][:, c, 8:16], i=t_: e.max(out=o, in_=i), ["tmp" + K_], ["m8b" + K_])
                if cut < 2:
                    continue
                P.op("dve", lambda e, o=i8[k][:, c, 0:8], m=m8[k][:, c, 0:8], i=s_: e.max_index(out=o, in_max=m, in_values=i),
                     ["sc" + K_, "m8a" + K_], ["i8a" + K_])
                P.op("dve", lambda e, o=i8[k][:, c, 8:16], m=m8[k][:, c, 8:16], i=t_: e.max_index(out=o, in_max=m, in_values=i),
                     ["tmp" + K_, "m8b" + K_], ["i8b" + K_])
            if cut < 3:
                continue
            P.copy("dve", i_f[k][:, :, :], i8[k][:, :, :], ["i8a" + K_, "i8b" + K_], ["if" + K_])
            v1 = m8[k][:, 0, :].unsqueeze(2).to_broadcast([128, 16, 16])
            v2 = m8[k][:, 1, :].unsqueeze(1).to_broadcast([128, 16, 16])
            P.tt("dve", cand[k][:, :, :], v1, v2, ALU.add, ["m8a" + K_, "m8b" + K_], ["cand" + K_])
            f1 = i_f[k][:, 0, :].unsqueeze(2).to_broadcast([128, 16, 16])
            f2 = i_f[k][:, 1, :].unsqueeze(1).to_broadcast([128, 16, 16])
            P.stt(cidx[k][:, :, :], f1, 128.0, f2, ALU.mult, ALU.add, ["if" + K_], ["cidx" + K_])
            if cut < 4:
                continue
            cf = cand[k][:, :, :].rearrange("p a b -> p (a b)")
            xf = cidx[k][:, :, :].rearrange("p a b -> p (a b)")
            P.op("dve", lambda e, o=b16[k][:, 0:8], i=cf: e.max(out=o, in_=i), ["cand" + K_], ["b16a" + K_])
            P.op("dve", lambda e, o=tmp[k][:, :], r=b16[k][:, 0:8], i=cf: e.match_replace(out=o, in_to_replace=r, in_values=i, imm_value=-1e30),
                 ["cand" + K_, "b16a" + K_], ["tmp" + K_])
            P.op("dve", lambda e, o=b16[k][:, 8:16], i=tmp[k][:, :]: e.max(out=o, in_=i), ["tmp" + K_], ["b16b" + K_])
            br = ["b16a" + K_, "b16b" + K_]
            if cut < 5:
                continue
            P.op("dve", lambda e, o=p16[k][:, 0:8], m=b16[k][:, 0:8], i=cf: e.max_index(out=o, in_max=m, in_values=i),
                 ["cand" + K_] + br, ["p16" + K_])
            P.op("dve", lambda e, o=p16[k][:, 8:16], m=b16[k][:, 8:16], i=tmp[k][:, :]: e.max_index(out=o, in_max=m, in_values=i),
                 ["tmp" + K_] + br, ["p16" + K_])
            P.copy("dve", p16f[k][:, :], p16[k][:, :], ["p16" + K_], ["p16f" + K_])
            for kk in range(16):
                P.stt(junk[k][:, :], iota[:, :], p16f[k][:, kk:kk + 1], xf, ALU.is_equal, ALU.mult,
                      ["iota", "cidx" + K_, "p16f" + K_], ["junk" + K_, "eif%d" % tt],
                      accum=eif[:, tt, h * 16 + kk:h * 16 + kk + 1])
            if cut < 6:
                continue
            P.ts("dve", sm[k][:, 0:1], b16[k][:, 0:1], -1.0, None, ALU.mult, None, br, ["sm0" + K_])
            P.act(ex[k][:, :], b16[k][:, :], AF.Exp, br + ["sm0" + K_], ["ex" + K_, "sm1" + K_],
                  bias=sm[k][:, 0:1], accum=sm[k][:, 1:2])
            P.op("dve", lambda e, o=sm[k][:, 2:3], i=sm[k][:, 1:2]: e.reciprocal(out=o, in_=i), ["sm1" + K_], ["sm2" + K_])
            P.ts("dve", gts[:, tt, h * 16:(h + 1) * 16], ex[k][:, :], sm[k][:, 2:3], None, ALU.mult, None,
                 ["ex" + K_, "sm2" + K_], ["gts%d" % tt])
    for tt in range(NT):
        P.copy("dve", eii[:, tt, :], eif[:, tt, :], ["eif%d" % tt], ["eii%d" % tt])
        P.ld(EI[tt * 128:(tt + 1) * 128, :], eii[:, tt, :], "eio", ["eii%d" % tt], ())
        P.ld(GT[tt * 128:(tt + 1) * 128, :], gts[:, tt, :], "gto", ["gts%d" % tt], ())
    P.emit()
    return nc


def build_peer_ffn():
    nc = new_nc()
    X = din(nc, "X", [T, D])
    EI = din(nc, "EI", [T, 128], I32)
    GT = din(nc, "GT", [T, 128])
    U = din(nc, "U", [SEQ, D])
    V = din(nc, "V", [SEQ, D])
    G = din(nc, "G", [128, D])
    B = din(nc, "B", [128, D])
    Y = dout(nc, "Y", [T, D])
    P = Prog(nc)
    NG = 6
    g_bc = P.sb("g_bc", [128, D])
    b_bc = P.sb("b_bc", [128, D])
    xt = [P.sb("x%d" % i, [128, D]) for i in range(2)]
    ei = [P.sb("ei%d" % i, [128, 128], I32) for i in range(2)]
    gt = [P.sb("gt%d" % i, [128, 128]) for i in range(2)]
    hh = [P.sb("hh%d" % i, [128, 128]) for i in range(2)]
    aa = [P.sb("aa%d" % i, [128, 128]) for i in range(2)]
    acc = [P.sb("acc%d" % i, [128, D]) for i in range(2)]
    o_t = [P.sb("o%d" % i, [128, D]) for i in range(2)]
    st = [P.sb("st%d" % i, [128, 8]) for i in range(2)]
    gb = [P.sb("gb%d" % i, [128, D]) for i in range(NG)]
    junk = P.sb("junk", [128, D])
    P.ld(g_bc[:, :], G[:, :], "g", (), ["gb"])
    P.ld(b_bc[:, :], B[:, :], "b", (), ["gb"])
    gi = 0

    def gather(dst, table, idx_ap, key, w):
        return P.dma("pool", lambda e: e.indirect_dma_start(
            out=dst, out_offset=None, in_=table,
            in_offset=bass.IndirectOffsetOnAxis(ap=idx_ap, axis=0)), key, ["ei"], w)

    for tt in range(T // 128):
        b = tt % 2
        B_ = str(b)
        P.ld(xt[b][:, :], X[tt * 128:(tt + 1) * 128, :], "x" + B_, (), ["x" + B_])
        o1 = P.ld(ei[b][:, :], EI[tt * 128:(tt + 1) * 128, :], "ei" + B_, (), ["ei" + B_])
        P.ld(gt[b][:, :], GT[tt * 128:(tt + 1) * 128, :], "gt" + B_, (), ["gt" + B_])
        for s_ in range(128):
            k = gi % NG
            gi += 1
            P.dma("pool", lambda e, d=gb[k][:, :], ia=ei[b][:, s_:s_ + 1]: e.indirect_dma_start(
                out=d, out_offset=None, in_=U[:, :], in_offset=bass.IndirectOffsetOnAxis(ap=ia, axis=0)),
                "gb%d" % k, ["ei" + B_], ["gb%d" % k])
            P.stt(junk[:, :], gb[k][:, :], 1.0, xt[b][:, :], ALU.mult, ALU.mult, ["gb%d" % k, "x" + B_],
                  ["junk", "hh" + B_], accum=hh[b][:, s_:s_ + 1])
        P.act(aa[b][:, :], hh[b][:, :], AF.Gelu, ["hh" + B_], ["aa" + B_])
        P.tt("dve", aa[b][:, :], aa[b][:, :], gt[b][:, :], ALU.mult, ["aa" + B_, "gt" + B_], ["aa" + B_])
        for s_ in range(128):
            k = gi % NG
            gi += 1
            P.dma("pool", lambda e, d=gb[k][:, :], ia=ei[b][:, s_:s_ + 1]: e.indirect_dma_start(
                out=d, out_offset=None, in_=V[:, :], in_offset=bass.IndirectOffsetOnAxis(ap=ia, axis=0)),
                "gb%d" % k, ["ei" + B_], ["gb%d" % k])
            if s_ == 0:
                P.ts("dve", acc[b][:, :], gb[k][:, :], aa[b][:, 0:1], None, ALU.mult, None,
                     ["gb%d" % k, "aa" + B_], ["acc" + B_])
            else:
                P.stt(acc[b][:, :], gb[k][:, :], aa[b][:, s_:s_ + 1], acc[b][:, :], ALU.mult, ALU.add,
                      ["gb%d" % k, "aa" + B_, "acc" + B_], ["acc" + B_])
        P.stt(acc[b][:, :], xt[b][:, :], ALPHA, acc[b][:, :], ALU.mult, ALU.add, ["x" + B_, "acc" + B_], ["acc" + B_])
        ln_tok(P, acc[b], ["acc" + B_], o_t[b], ["o" + B_], g_bc, b_bc, junk, st[b], B_)
        P.ld(Y[tt * 128:(tt + 1) * 128, :], o_t[b][:, :], "o" + B_, ["o" + B_], ())
    P.emit()
    return nc


def build_attn(groups, nkv, sinks, moba):
    nc = new_nc()
    ng = len(groups)
    nq = 16
    TK = [T + 128 * r for r in groups]
    NTK = [tk // 128 for tk in TK]
    QT = din(nc, "QT", [nq, ng, 64, T])
    KT = [din(nc, "KT%d" % g, [nkv, 64, TK[g]]) for g in range(ng)]
    VA = [din(nc, "VA%d" % g, [nkv, 128, NTK[g], 64]) for g in range(ng)]
    MK = din(nc, "MK", [nq, 128, ng * 3 * 256])
    SH = din(nc, "SH", [128, 64])
    SK = din(nc, "SK", [128, 16]) if sinks else None
    if moba:
        K2 = din(nc, "K2", [nq, 64, SEQ])
        V2 = din(nc, "V2", [nq, 128, 128, 64])
        PNEG = din(nc, "PNEG", [128, 8 * 64])
        PONE = din(nc, "PONE", [128, 8 * 64])
        BIASC = din(nc, "BIASC", [nq, 128, 8 * 64 * 2])
        ROWF = din(nc, "ROWF", [nq, 128, 256])
        IDN = din(nc, "IDN", [128, 128])
    OT = dout(nc, "OT", [D, T])
    P = Prog(nc)
    q_sb = [P.sb("q%d" % g, [128, T]) for g in range(ng)]
    k_sb = [P.sb("k%d" % g, [64, TK[g]]) for g in range(ng)]
    v_sb = [P.sb("v%d" % g, [128, NTK[g], 128]) for g in range(ng)]
    mk = P.sb("mk", [128, ng * 3 * 256])
    sh = P.sb("sh", [128, 64])
    U = P.sb("U", [128, T])
    R = P.sb("R", [64, T])
    Oh = [P.sb("Oh%d" % i, [64, T]) for i in range(2)]
    pt = [P.sb("pt%d" % i, [128, 256]) for i in range(3)]
    ps_s = [P.psum("ps_s%d" % i) for i in range(2)]
    ps_o = [P.psum("ps_o%d" % i) for i in range(2)]
    ps_d = P.psum("ps_d")
    P.ld(sh[:, :], SH[:, :], "sh", (), ["sh"])
    for g in range(ng):
        P.memset("pool", v_sb[g][:, :, 64:128], 1.0, ["v%d" % g])
    if sinks:
        sk = P.sb("sk", [128, 16])
        esk = P.sb("esk", [128, 16])
        P.ld(sk[:, :], SK[:, :], "sk", (), ["sk"])
        P.act(esk[:, :], sk[:, :], AF.Exp, ["sk"], ["esk"])
    if moba:
        k2 = P.sb("k2", [64, SEQ])
        v2 = P.sb("v2", [128, 128, 64])
        ones = P.sb("ones", [128, 128])
        idn = P.sb("idn", [128, 128])
        pneg = P.sb("pneg", [128, 512])
        pone = P.sb("pone", [128, 512])
        biasc = P.sb("biasc", [128, 1024])
        rowf = P.sb("rowf", [128, 256])
        km = P.sb("km", [64, 64])
        gm = P.sb("gm", [128, 64])
        m8 = P.sb("m8", [128, 8])
        sel = P.sb("sel", [128, 64])
        nmT = P.sb("nmT", [64, 256])
        ut = P.sb("ut", [128, 256])
        ps_u = P.psum("ps_u")
        ps_ud = P.psum("ps_ud")
        ps_g = P.psum("ps_g")
        P.memset("pool", ones[:, :], 1.0, ["ones"])
        P.ld(idn[:, :], IDN[:, :], "idn", (), ["idn"])
        P.ld(pneg[:, :], PNEG[:, :], "pneg", (), ["pneg"])
        P.ld(pone[:, :], PONE[:, :], "pone", (), ["pone"])
    si = 0
    oi = 0
    pi = 0
    cur_kv = -1
    for h in range(nq):
        kv = h * nkv // nq
        H_ = str(h % 2)
        if kv != cur_kv:
            cur_kv = kv
            for g in range(ng):
                P.ld(k_sb[g][:, :], KT[g][kv, :, :], "k%d" % g, (), ["k%d" % g])
                P.ld(v_sb[g][:, :, 0:64], VA[g][kv, :, :, :], "v%d" % g, (), ["v%d" % g])
        for g in range(ng):
            P.ld(q_sb[g][0:64, :], QT[h, g, :, :], "q%d" % g, (), ["q%d" % g])
        P.ld(mk[:, :], MK[h, :, :], "mk", (), ["mk"])
        if moba:
            P.ld(k2[:, :], K2[h, :, :], "k2", (), ["k2"])
            for vq in range(8):
                P.ld(v2[:, vq * 16:(vq + 1) * 16, :], V2[h, :, vq * 16:(vq + 1) * 16, :], "v2_%d" % vq, (), ["v2"])
            P.ld(biasc[:, :], BIASC[h, :, :], "biasc", (), ["biasc"])
            P.ld(rowf[:, :], ROWF[h, :, :], "rowf", (), ["rowf"])
        for g, r in enumerate(groups):
            G_ = str(g)
            nbk = T // r // 128
            for cc in range(r):
                for j in range(nbk):
                    qc = cc * (T // r) + j * 128
                    tA = (cc * (T // r + 128) + j * 128) // 128
                    kind = (2 if j % 2 == 0 else 0) if moba else (1 if j == 0 else 0)
                    s_ = si % 2
                    si += 1
                    o_ = oi % 2
                    oi += 1
                    p_ = pi % 3
                    pi += 1
                    for t in range(2):
                        P.mm(ps_s[s_][:, t * 128:(t + 1) * 128], k_sb[g][0:64, (tA + t) * 128:(tA + t + 1) * 128],
                             q_sb[g][0:64, qc:qc + 128], True, True, ["k" + G_, "q" + G_], ["ps_s%d" % s_])
                    P.act(pt[p_][:, :], ps_s[s_][:, 0:256], AF.Exp, ["ps_s%d" % s_], ["pt%d" % p_], scale=0.125)
                    mo = (g * 3 + kind) * 256
                    P.tt("pool" if si % 2 else "dve", pt[p_][:, :], pt[p_][:, :], mk[:, mo:mo + 256], ALU.mult,
                         ["pt%d" % p_, "mk"], ["pt%d" % p_])
                    for t in range(2):
                        P.mm(ps_o[o_][:, 0:128], v_sb[g][:, tA + t, :], pt[p_][:, t * 128:(t + 1) * 128],
                             t == 0, t == 1, ["v" + G_, "pt%d" % p_], ["ps_o%d" % o_])
                    u0 = cc + r * j * 128
                    ucols = U[:, u0:u0 + r * 127 + 1:r]
                    if g == 0:
                        P.copy("act", ucols, ps_o[o_][:, 0:128], ["ps_o%d" % o_], ["U"])
                    else:
                        P.tt("dve", ucols, ucols, ps_o[o_][:, 0:128], ALU.add, ["ps_o%d" % o_, "U"], ["U"])
        if moba:
            P.op("dve", lambda e: e.tensor_reduce(out=km[:, :], in_=k2[:, :].rearrange("p (n j) -> p n j", j=256),
                                                  axis=AX.X, op=ALU.add), ["k2"], ["km"])
            for qb in range(8):
                for qh in range(2):
                    qsl = slice(qb * 256 + qh * 128, qb * 256 + qh * 128 + 128)
                    P.mm(ps_g[:, 0:64], q_sb[0][0:64, qsl], km[0:64, :], True, True, ["q0", "km"], ["ps_g"])
                    P.tt("dve", gm[:, :], ps_g[:, 0:64], pneg[:, qb * 64:(qb + 1) * 64], ALU.add, ["ps_g", "pneg"], ["gm"])
                    P.op("dve", lambda e: e.max(out=m8[:, :], in_=gm[:, :]), ["gm"], ["m8"])
                    P.ts("dve", sel[:, 0:64], gm[:, :], m8[:, 2:3], None, ALU.is_ge, None, ["gm", "m8"], ["sel"])
                    P.tt("dve", sel[:, 0:64], sel[:, 0:64], pone[:, qb * 64:(qb + 1) * 64], ALU.mult, ["sel", "pone"], ["sel"])
                    P.ts("dve", sel[:, 0:64], sel[:, 0:64], 240000.0, -240000.0, ALU.mult, ALU.add, ["sel"], ["sel"])
                    P.tr(ps_g[0:64, 128:256], sel[:, :], idn[:, :], ["sel", "idn"], ["ps_g"])
                    P.copy("act", nmT[:, qh * 128:(qh + 1) * 128], ps_g[0:64, 128:256], ["ps_g"], ["nmT"])
                npast = 56 + qb
                qs2 = slice(qb * 256, (qb + 1) * 256)
                for n in range(npast):
                    for t in range(2):
                        ktg = 2 * n + t
                        kc0 = ktg * 128
                        s_ = si % 2
                        si += 1
                        p_ = pi % 3
                        pi += 1
                        P.mm(ps_s[s_][:, 0:256], k2[0:64, kc0:kc0 + 128], q_sb[0][0:64, qs2], True, False,
                             ["k2", "q0"], ["ps_s%d" % s_])
                        P.mm(ps_s[s_][:, 0:256], idn[0:64, n:n + 1].to_broadcast([64, 128]), nmT[0:64, :], False, True,
                             ["idn", "nmT"], ["ps_s%d" % s_])
                        bcol = (qb * 64 + n) * 2 + t
                        P.act(pt[p_][:, :], ps_s[s_][:, 0:256], AF.Exp, ["ps_s%d" % s_, "biasc"], ["pt%d" % p_],
                              scale=0.125, bias=biasc[:, bcol:bcol + 1])
                        P.mm(ps_u[0:64, 0:256], v2[:, ktg, :], pt[p_][:, :], n == 0 and t == 0, n == npast - 1 and t == 1,
                             ["v2", "pt%d" % p_], ["ps_u"])
                        P.mm(ps_ud[:, 0:256], ones[:, :], pt[p_][:, :], n == 0 and t == 0, n == npast - 1 and t == 1,
                             ["ones", "pt%d" % p_], ["ps_ud"])
                P.tt("dve", ut[0:64, :], ps_u[0:64, 0:256], rowf[0:64, :], ALU.mult, ["ps_u", "rowf"], ["ut"])
                P.tt("dve", ut[64:128, :], ps_ud[64:128, 0:256], rowf[64:128, :], ALU.mult, ["ps_ud", "rowf", "ut"], ["ut"])
                P.tt("dve", U[:, qs2], U[:, qs2], ut[:, :], ALU.add, ["ut", "U"], ["U"])
        if sinks:
            P.ts("dve", U[64:128, :], U[64:128, :], esk[64:128, h:h + 1], None, ALU.add, None, ["U", "esk"], ["U"])
        for tg in range(T // 512):
            tsl = slice(tg * 512, (tg + 1) * 512)
            P.mm(ps_d[0:64, :], sh[:, :], U[:, tsl], True, True, ["sh", "U"], ["ps_d"])
            P.op("dve", lambda e, o=R[:, tsl]: e.reciprocal(out=o, in_=ps_d[0:64, :]), ["ps_d"], ["R"])
        P.tt("dve", Oh[h % 2][:, :], U[0:64, :], R[:, :], ALU.mult, ["U", "R"], ["Oh" + H_])
        P.ld(OT[h * 64:(h + 1) * 64, :], Oh[h % 2][:, :], "Oh" + H_, ["Oh" + H_], ())
    P.emit()
    return nc


def _slopes():
    return 2.0 ** (-8.0 * np.arange(1, 17, dtype=np.float64) / 16.0)


def _mask_tables(gmd, core):
    ng = len(gmd)
    k = np.arange(128)[:, None]
    q = np.arange(128)[None, :]
    out = np.zeros((16, 128, ng, 3, 256), np.float32)
    sl = _slopes()
    for h in range(16):
        for g, (r, md) in enumerate(gmd):
            dp = q + 128 - k
            do = q - k
            mp = np.where((dp >= 0) & (dp <= md), np.exp(-sl[h] * r * dp), 0.0)
            mo = np.where((do >= 0) & (do <= md), np.exp(-sl[h] * r * do), 0.0)
            out[h, :, g, 0, :128] = mp
            out[h, :, g, 0, 128:] = mo
            if core > 0:
                out[h, :, g, 1, :128] = mp
            out[h, :, g, 1, 128:] = mo
            out[h, :, g, 2, 128:] = mo
    return out.reshape(16, 128, ng * 3 * 256)


def _perm_idx(c, r):
    L = T // r
    qi = np.concatenate([c * T + cc + r * np.arange(L) for cc in range(r)])
    ki = np.concatenate([cc + r * (c * L - 128 + np.arange(L + 128)) for cc in range(r)])
    return qi, ki


def _take_cols(a, idx):
    out = a[..., np.maximum(idx, 0)]
    out[..., idx < 0] = 0.0
    return np.ascontiguousarray(out)


def _attention(kind, yT, v, sinks):
    if kind == 0:
        groups, gmd, nkv = [1], [(1, 127)], 2
        qrow = lambda g, h: h * 64
        krow = lambda kv: 1024 + kv * 64
    elif kind == 1:
        groups, gmd, nkv = [1, 4, 16], [(1, 128), (4, 128), (16, 128)], 16
        qrow = lambda g, h: g * 1024 + h * 64
        krow = lambda kv: 3072 + kv * 64
    else:
        groups, gmd, nkv = [1], [(1, 255)], 16
        qrow = lambda g, h: h * 64
        krow = lambda kv: 1024 + kv * 64
    ng = len(groups)
    sh = np.zeros((128, 64), np.float32)
    sh[np.arange(64) + 64, np.arange(64)] = 1.0
    Kh = [yT[krow(kv):krow(kv) + 64] for kv in range(nkv)]
    Vh = [v[:, kv * 64:(kv + 1) * 64] for kv in range(nkv)]
    shared = {}
    if kind == 2:
        shared["K2"] = np.ascontiguousarray(np.stack([Kh[h] for h in range(16)]))
        shared["V2"] = np.ascontiguousarray(np.stack(
            [Vh[h].reshape(128, 128, 64).transpose(1, 0, 2) for h in range(16)]))
        shared["IDN"] = np.eye(128, dtype=np.float32)
        sl = _slopes()
        shared["ROWF"] = np.ascontiguousarray(np.broadcast_to(
            np.exp(-sl[:, None, None] * np.arange(256)[None, None, :]), (16, 128, 256)).astype(np.float32))
    ims = []
    for c in range(NCORES):
        m = {"SH": sh, "MK": _mask_tables(gmd, c)}
        QT = np.zeros((16, ng, 64, T), np.float32)
        for g, r in enumerate(groups):
            qi, ki = _perm_idx(c, r)
            for h in range(16):
                QT[h, g] = yT[qrow(g, h):qrow(g, h) + 64][:, qi]
            m["KT%d" % g] = np.stack([_take_cols(Kh[kv], ki) for kv in range(nkv)])
            nt = len(ki) // 128
            va = np.stack([_take_cols(np.ascontiguousarray(Vh[kv].T), ki).T.reshape(nt, 128, 64).transpose(1, 0, 2)
                           for kv in range(nkv)])
            m["VA%d" % g] = np.ascontiguousarray(va)
        m["QT"] = QT
        if kind == 0:
            m["SK"] = np.ascontiguousarray(np.broadcast_to(sinks[None, :], (128, 16)).astype(np.float32))
        if kind == 2:
            m.update(shared)
            nidx = np.arange(64)[None, :]
            qb = np.arange(8)[:, None]
            past = nidx < (8 * c + qb)
            m["PNEG"] = np.ascontiguousarray(np.broadcast_to(
                np.where(past, 0.0, -1e30).reshape(1, 512), (128, 512)).astype(np.float32))
            m["PONE"] = np.ascontiguousarray(np.broadcast_to(
                past.astype(np.float32).reshape(1, 512), (128, 512)))
            sl = _slopes()
            delta = (8 * c + qb - nidx)[None, None, :, :, None]
            pp = np.arange(128)[None, :, None, None, None]
            tt = np.arange(2)[None, None, None, None, :]
            bias = -sl[:, None, None, None, None] * (256.0 * delta - 128.0 * tt - pp)
            bias = np.where(past[None, None, :, :, None], bias, 0.0)
            m["BIASC"] = np.ascontiguousarray(bias.reshape(16, 128, 1024).astype(np.float32))
        ims.append(m)
    key = ("attn", kind)
    res = run(key, lambda: build_attn(groups, nkv, kind == 0, kind == 2), ims)
    return [r["OT"] for r in res]


def _tile128(vec):
    return np.ascontiguousarray(np.broadcast_to(vec[None, :], (128, vec.shape[0])).astype(np.float32))


def kernel(x, p, a_w_qkv, a_sinks, a_w_o, b_w_qkv, b_w_o, c_w_qkv, c_w_o,
           ln1_g, ln1_b, ln2_g, ln2_b, peer_w_q, peer_subkeys, peer_u, peer_v,
           ple_w, ple_gate_w, ple_gate_b, _nlayers=DEPTH, _debug=None):
    f = lambda a: np.ascontiguousarray(np.asarray(a, dtype=np.float32))
    x_tm = f(x)[0]
    xT = [np.ascontiguousarray(x_tm[c * T:(c + 1) * T].T) for c in range(NCORES)]
    cs = lambda a, c: a[c * T:(c + 1) * T]
    for i in range(_nlayers):
        kind, j = i % 3, i // 3
        if kind == 0:
            W, Wo, nf, nv = f(a_w_qkv[j]), f(a_w_o[j]), 1152, 128
        elif kind == 1:
            W, Wo, nf, nv = f(b_w_qkv[j]), f(b_w_o[j]), 4096, 1024
        else:
            W, Wo, nf, nv = f(c_w_qkv[j]), f(c_w_o[j]), 2048, 1024
        res = run(("proj", nf, nv), lambda: build_proj(D, nf, nv), [{"xT": xT[c], "W": W} for c in range(NCORES)])
        yT = np.concatenate([r["yT"] for r in res], axis=1)
        v = np.concatenate([r["v"] for r in res], axis=0)
        OT = _attention(kind, yT, v, f(a_sinks[j]) if kind == 0 else None)
        if _debug is not None:
            _debug["OT%d" % i] = OT
        g1, b1 = _tile128(f(ln1_g[i])), _tile128(f(ln1_b[i]))
        res = run("post", build_post, [{"OT": OT[c], "X": cs(x_tm, c), "Wo": Wo, "G": g1, "B": b1}
                                       for c in range(NCORES)])
        x1 = np.concatenate([r["Y"] for r in res], axis=0)
        if _debug is not None:
            _debug["x1_%d" % i] = x1
        skt = np.ascontiguousarray(f(peer_subkeys[i]).transpose(2, 0, 1).reshape(128, 256))
        wq = f(peer_w_q[i])
        iota_ = _tile128(np.arange(256, dtype=np.float32))
        res = run("psel", build_peer_sel, [{"XT": np.ascontiguousarray(cs(x1, c).T), "Wq": wq, "SKT": skt, "IOTA": iota_}
                                           for c in range(NCORES)])
        EI = [r["EI"] for r in res]
        GT = [r["GT"] for r in res]
        g2, b2 = _tile128(f(ln2_g[i])), _tile128(f(ln2_b[i]))
        u_, v_ = f(peer_u[i]), f(peer_v[i])
        res = run("pffn", build_peer_ffn, [{"X": cs(x1, c), "EI": EI[c], "GT": GT[c], "U": u_, "V": v_,
                                            "G": g2, "B": b2} for c in range(NCORES)])
        x2 = [r["Y"] for r in res]
        if _debug is not None:
            _debug["x2_%d" % i] = np.concatenate(x2, 0)
        pi_ = f(p[i])[0]
        bgm = np.ascontiguousarray(f(ple_gate_b[i]).reshape(8, 128).T)
        res = run("ple", build_ple, [{"XT": np.ascontiguousarray(x2[c].T), "PT": np.ascontiguousarray(cs(pi_, c).T),
                                      "Wg": f(ple_gate_w[i]), "Wp": f(ple_w[i]), "BG": bgm} for c in range(NCORES)])
        xT = [r["YT"] for r in res]
        x_tm = np.concatenate([t.T for t in xT], axis=0)
        if _debug is not None:
            _debug["x3_%d" % i] = x_tm
    return np.ascontiguousarray(x_tm[None]).astype(np.float32)
```
